# Optimizing a Trainium2 kernel written in Bass

```python
import math
import jax, jax.numpy as jnp
from jax import lax
import numpy as np

D_MODEL = 4096
BATCH = 2
SEQ = 8192
DEPTH = 4

HEAD_DIM = 128
N_HEADS = D_MODEL // HEAD_DIM
HEADS_NA = N_HEADS // 4
HEADS_DIL = 3 * N_HEADS // 8
HEADS_GQA = N_HEADS - HEADS_NA - HEADS_DIL
KV_HEADS_GQA = HEADS_GQA // 3
MIX_WIDTH = N_HEADS * HEAD_DIM
KV_WIDTH = (HEADS_NA + HEADS_DIL + KV_HEADS_GQA) * HEAD_DIM
D_FF = 4 * D_MODEL
GRID_W = 64
NA_ROWS = 8
NA_COLS = 16
DIL_PATTERNS = ((128, 1), (512, 4), (2048, 16))
T5_BUCKETS = 32
T5_MAX_DIST = 1024
ROPE_THETA = 10000.0
QUERY_BLOCK = 128
N_MOD = 6
EPS = 1e-6
NEG_INF = -1e30
ATTN_SCALE = HEAD_DIM ** -0.5

kernel_name = "hybrid_parallel_heads_bidir_encoder"


def rms_norm(x, g):
    xf = x.astype(jnp.float32)
    y = xf * lax.rsqrt(jnp.mean(xf * xf, axis=-1, keepdims=True) + EPS)
    return (y * g.astype(jnp.float32)).astype(x.dtype)


def modulate(h, shift, scale):
    return h * (1.0 + scale[:, None, :]) + shift[:, None, :]


def t5_bucket(rel):
    nb = T5_BUCKETS // 2
    max_exact = nb // 2
    ret = jnp.where(rel > 0, nb, 0)
    n = jnp.abs(rel)
    nf = jnp.maximum(n, 1).astype(jnp.float32)
    large = max_exact + (jnp.log(nf / max_exact) / math.log(T5_MAX_DIST / max_exact)
                         * (nb - max_exact)).astype(jnp.int32)
    large = jnp.minimum(large, nb - 1)
    return ret + jnp.where(n < max_exact, n, large)


def axial_rope(x):
    S = x.shape[1]
    t = jnp.arange(S)
    half = HEAD_DIM // 2
    inv = jnp.exp(-math.log(ROPE_THETA) * jnp.arange(0, half, 2, dtype=jnp.float32) / half)

    def rot(xa, pos):
        ang = pos.astype(jnp.float32)[:, None] * inv[None, :]
        cos = jnp.cos(ang)[None, :, None, :]
        sin = jnp.sin(ang)[None, :, None, :]
        x1, x2 = jnp.split(xa.astype(jnp.float32), 2, axis=-1)
        return jnp.concatenate([x1 * cos - x2 * sin, x1 * sin + x2 * cos], axis=-1)

    out = jnp.concatenate([rot(x[..., :half], t // GRID_W), rot(x[..., half:], t % GRID_W)], axis=-1)
    return out.astype(x.dtype)


def neighborhood_attention(q, k, v, rpb):
    Bn, H, S, hd = q.shape
    rows = S // GRID_W
    kr = min(NA_ROWS, rows)
    qg = q.reshape(Bn, H, rows, GRID_W, hd)
    kg = k.reshape(Bn, H, rows, GRID_W, hd)
    vg = v.reshape(Bn, H, rows, GRID_W, hd)
    col = jnp.arange(GRID_W)
    cs = jnp.clip(col - NA_COLS // 2, 0, GRID_W - NA_COLS)
    cidx = cs[:, None] + jnp.arange(NA_COLS)[None, :]
    coff = cidx - col[:, None] + NA_COLS - 1

    def row_fn(args):
        q_row, r = args
        rs = jnp.clip(r - kr // 2, 0, rows - kr)
        k_rows = lax.dynamic_slice_in_dim(kg, rs, kr, axis=2)
        v_rows = lax.dynamic_slice_in_dim(vg, rs, kr, axis=2)
        k_nb = k_rows[:, :, :, cidx, :]
        v_nb = v_rows[:, :, :, cidx, :]
        roff = rs + jnp.arange(kr) - r + NA_ROWS - 1
        bias = rpb[:, roff[None, :, None], coff[:, None, :]]
        s = jnp.einsum('bhqd,bhrqcd->bhqrc', q_row, k_nb).astype(jnp.float32) + bias.astype(jnp.float32)
        p = jax.nn.softmax(s.reshape(Bn, H, GRID_W, kr * NA_COLS), axis=-1).reshape(s.shape)
        return jnp.einsum('bhqrc,bhrqcd->bhqd', p.astype(v.dtype), v_nb)

    o = lax.map(row_fn, (qg.transpose(2, 0, 1, 3, 4), jnp.arange(rows)))
    return o.transpose(1, 2, 0, 3, 4).reshape(Bn, H, S, hd)


def dilated_window_attention(q, k, v, t5_table, d, half):
    Bn, H, S, hd = q.shape
    L = S // d
    qb = math.gcd(L, QUERY_BLOCK)
    nb = L // qb
    win = qb + 2 * half

    def regroup(a):
        return a.reshape(Bn, H, L, d, hd).transpose(0, 1, 3, 2, 4)

    qr = regroup(q).reshape(Bn, H, d, nb, qb, hd)
    pad_cfg = ((0, 0), (0, 0), (0, 0), (half, half), (0, 0))
    kp = jnp.pad(regroup(k), pad_cfg)
    vp = jnp.pad(regroup(v), pad_cfg)
    idx = (jnp.arange(nb) * qb)[:, None] + jnp.arange(win)[None, :]
    kw = kp[:, :, :, idx, :]
    vw = vp[:, :, :, idx, :]
    j = jnp.arange(win)[None, :] - half - jnp.arange(qb)[:, None]
    key_pos = idx - half
    valid = ((jnp.abs(j) <= half)[None]
             & (key_pos >= 0)[:, None, :] & (key_pos < L)[:, None, :])
    bias = t5_table[t5_bucket(j * d)].transpose(2, 0, 1).astype(jnp.float32)
    s = jnp.einsum('bhrnqd,bhrnkd->bhrnqk', qr, kw).astype(jnp.float32) + bias[None, :, None, None]
    s = jnp.where(valid[None, None, None], s, NEG_INF)
    m = jnp.max(s, axis=-1, keepdims=True)
    p = jnp.exp(s - m)
    den = jnp.sum(p, axis=-1, keepdims=True)
    o = jnp.einsum('bhrnqk,bhrnkd->bhrnqd', p, vw.astype(jnp.float32)) / den
    lse = (m + jnp.log(den))[..., 0]
    o = o.reshape(Bn, H, d, L, hd).transpose(0, 1, 3, 2, 4).reshape(Bn, H, S, hd)
    lse = lse.reshape(Bn, H, d, L).transpose(0, 1, 3, 2).reshape(Bn, H, S)
    return o, lse


def dilated_mixture_attention(q, k, v, t5_table):
    outs, lses = [], []
    for window, d in DIL_PATTERNS:
        o, lse = dilated_window_attention(q, k, v, t5_table, d, window // (2 * d))
        outs.append(o)
        lses.append(lse)
    w = jax.nn.softmax(jnp.stack(lses, axis=0), axis=0)
    return jnp.sum(w[..., None] * jnp.stack(outs, axis=0), axis=0)


def gqa_attention(q, k, v):
    Bn, H, S, hd = q.shape
    kvh = k.shape[1]
    grp = H // kvh
    nblk = S // QUERY_BLOCK
    qblocks = q.reshape(Bn, kvh, grp, nblk, QUERY_BLOCK, hd).transpose(3, 0, 1, 2, 4, 5)

    def block(q_blk):
        s = jnp.einsum('bkgqd,bksd->bkgqs', q_blk, k).astype(jnp.float32)
        p = jax.nn.softmax(s, axis=-1)
        return jnp.einsum('bkgqs,bksd->bkgqd', p.astype(v.dtype), v)

    o = lax.map(block, qblocks)
    return o.transpose(1, 2, 3, 0, 4, 5).reshape(Bn, H, S, hd)


def token_mixer(h, w_in, w_out, q_gain, k_gain, na_rpb, t5_table):
    Bn, S, _ = h.shape
    qkv = h @ w_in
    q, k, v = jnp.split(qkv, [MIX_WIDTH, MIX_WIDTH + KV_WIDTH], axis=-1)
    cut = [HEADS_NA * HEAD_DIM, (HEADS_NA + HEADS_DIL) * HEAD_DIM]
    q_na, q_dil, q_gqa = jnp.split(q, cut, axis=-1)
    k_na, k_dil, k_gqa = jnp.split(k, cut, axis=-1)
    v_na, v_dil, v_gqa = jnp.split(v, cut, axis=-1)

    def heads(a):
        return a.reshape(Bn, S, -1, HEAD_DIM).transpose(0, 2, 1, 3)

    o_na = neighborhood_attention(heads(q_na * ATTN_SCALE), heads(k_na), heads(v_na), na_rpb)
    o_dil = dilated_mixture_attention(heads(q_dil * ATTN_SCALE), heads(k_dil), heads(v_dil), t5_table)
    qg = axial_rope(rms_norm(q_gqa.reshape(Bn, S, HEADS_GQA, HEAD_DIM), q_gain)) * ATTN_SCALE
    kg = axial_rope(rms_norm(k_gqa.reshape(Bn, S, KV_HEADS_GQA, HEAD_DIM), k_gain))
    o_gqa = gqa_attention(qg.transpose(0, 2, 1, 3), kg.transpose(0, 2, 1, 3), heads(v_gqa))

    def merge(o):
        return o.transpose(0, 2, 1, 3).reshape(Bn, S, -1).astype(h.dtype)

    o = jnp.concatenate([merge(o_na), merge(o_dil), merge(o_gqa)], axis=-1)
    return o @ w_out


def squared_relu_mlp(h, w1, w2):
    return jnp.square(jax.nn.relu(h @ w1)) @ w2


def setup_inputs(seed: int = 0) -> dict:
    key = jax.random.key(seed)
    ks = jax.random.split(key, 14)
    f32 = jnp.float32
    nrm = jax.random.normal
    x = nrm(ks[0], (BATCH, SEQ, D_MODEL), f32)
    c = nrm(ks[1], (BATCH, D_MODEL), f32)
    ada_w = nrm(ks[2], (D_MODEL, N_MOD * D_MODEL), f32) * D_MODEL ** -0.5
    ada_b = 0.01 * nrm(ks[3], (N_MOD * D_MODEL,), f32)
    ada_layer_emb = 0.1 * nrm(ks[4], (DEPTH, N_MOD, D_MODEL), f32)
    norm_gains = 1.0 + 0.05 * nrm(ks[5], (DEPTH, 4, D_MODEL), f32)
    w_in = nrm(ks[6], (DEPTH, D_MODEL, MIX_WIDTH + 2 * KV_WIDTH), f32) * D_MODEL ** -0.5
    w_out = nrm(ks[7], (DEPTH, MIX_WIDTH, D_MODEL), f32) * MIX_WIDTH ** -0.5
    q_gain = 1.0 + 0.05 * nrm(ks[8], (DEPTH, HEAD_DIM), f32)
    k_gain = 1.0 + 0.05 * nrm(ks[9], (DEPTH, HEAD_DIM), f32)
    na_rpb = 0.2 * nrm(ks[10], (DEPTH, HEADS_NA, 2 * NA_ROWS - 1, 2 * NA_COLS - 1), f32)
    t5_table = 0.2 * nrm(ks[11], (T5_BUCKETS, HEADS_DIL), f32)
    w_mlp_in = nrm(ks[12], (DEPTH, D_MODEL, D_FF), f32) * D_MODEL ** -0.5
    w_mlp_out = nrm(ks[13], (DEPTH, D_FF, D_MODEL), f32) * D_FF ** -0.5
    return {"x": x, "c": c, "ada_w": ada_w, "ada_b": ada_b, "ada_layer_emb": ada_layer_emb,
            "norm_gains": norm_gains, "w_in": w_in, "w_out": w_out, "q_gain": q_gain,
            "k_gain": k_gain, "na_rpb": na_rpb, "t5_table": t5_table,
            "w_mlp_in": w_mlp_in, "w_mlp_out": w_mlp_out}


def reference(x, c, ada_w, ada_b, ada_layer_emb, norm_gains, w_in, w_out, q_gain, k_gain,
              na_rpb, t5_table, w_mlp_in, w_mlp_out):
    mod = (jax.nn.silu(c) @ ada_w + ada_b).reshape(c.shape[0], N_MOD, D_MODEL)
    for l in range(DEPTH):
        m = mod + ada_layer_emb[l][None]
        sh1, sc1, g1, sh2, sc2, g2 = m[:, 0], m[:, 1], m[:, 2], m[:, 3], m[:, 4], m[:, 5]
        h = modulate(rms_norm(x, norm_gains[l, 0]), sh1, sc1)
        y = token_mixer(h, w_in[l], w_out[l], q_gain[l], k_gain[l], na_rpb[l], t5_table)
        x = x + g1[:, None, :] * rms_norm(y, norm_gains[l, 1])
        h = modulate(rms_norm(x, norm_gains[l, 2]), sh2, sc2)
        y = squared_relu_mlp(h, w_mlp_in[l], w_mlp_out[l])
        x = x + g2[:, None, :] * rms_norm(y, norm_gains[l, 3])
    return x
```

```python
import math
import numpy as np
import ml_dtypes
import concourse.bass as bass
import concourse.mybir as mybir
from concourse.bass_utils import run_bass_kernel_spmd

F32 = mybir.dt.float32
BF16 = mybir.dt.bfloat16
I32 = mybir.dt.int32
AF = mybir.ActivationFunctionType
ALU = mybir.AluOpType

EPS = 1e-6
HD = 128
ATTN_SCALE = HD ** -0.5
GRID_W = 64
NQ = 512
NV = 384
DIL_KBMIN, DIL_KBMAX = -8, 11
DIL_J = NQ + 128 * (DIL_KBMAX - DIL_KBMIN)
NA_NDD = 24
NA_COLS_T = NA_NDD * 64


class Cfg:
    def __init__(s, D, S, B, DFF, DEPTH):
        s.D, s.S, s.B, s.DFF, s.DEPTH = D, S, B, DFF, DEPTH
        s.GRP = 8 // B
        s.KC = D // 128
        s.TOK = S // s.GRP
        s.NTG = s.TOK // NQ
        s.FC = DFF // 128
        s.NKG2 = s.FC // s.KC
        s.NH = D // 128
        s.H_NA = s.NH // 4
        s.H_DIL = 3 * s.NH // 8
        s.H_GQA = s.NH - s.H_NA - s.H_DIL
        s.H_KVG = s.H_GQA // 3
        assert s.H_NA == 2 * s.GRP and s.H_DIL == 3 * s.GRP and s.H_KVG == s.GRP
        s.KVW = (s.H_NA + s.H_DIL + s.H_KVG) * 128
        s.G = S // NQ
        s.ROWS = S // GRID_W
        s.NA6 = 6 * D // 8
        s.MQK = s.GRP * 14
        s.NVG = 2 * s.GRP
        assert s.G >= 3 and s.NA6 % 512 == 0
        KC = s.KC
        s.wshapes = {"wqk": (s.MQK * 128, KC * 128), "wv": (s.NVG * 128, KC * NV), "wo": (KC * 128, KC * 128),
                     "w1": (s.FC * 128, KC * 128), "w2": (KC * s.NKG2 * 128, KC * 128)}
        s.wrp = {}
        for k, (R, C) in s.wshapes.items():
            rp = 1
            while rp * 2 * C * 2 <= 256 * 1024 and R % (8 * rp * 2) == 0:
                rp *= 2
            s.wrp[k] = rp


FULL = Cfg(4096, 8192, 2, 16384, 4)


def _q_heads(c, j):
    return [2 * j, 2 * j + 1] + [c.H_NA + 3 * j + i for i in range(3)] + \
           [c.H_NA + c.H_DIL + 3 * j + i for i in range(3)]


def _kv_heads(c, j):
    return [2 * j, 2 * j + 1] + [c.H_NA + 3 * j + i for i in range(3)] + [c.H_NA + c.H_DIL + j]


def _tileB(W, KC):
    K, N = W.shape
    M = N // 128
    NKG = K // (KC * 128)
    t = W.reshape(NKG, KC, 128, M, 128).transpose(3, 0, 2, 1, 4)
    return np.ascontiguousarray(t).reshape(M * NKG * 128, KC * 128)


def _tileV(W, KC):
    K, N = W.shape
    NVG = N // NV
    t = W.reshape(KC, 128, NVG, NV).transpose(2, 1, 0, 3)
    return np.ascontiguousarray(t).reshape(NVG * 128, KC * NV)


def _t5_bucket(rel):
    nb, max_exact = 16, 8
    ret = np.where(rel > 0, nb, 0)
    n = np.abs(rel)
    nf = np.maximum(n, 1).astype(np.float32)
    large = max_exact + (np.log(nf / np.float32(max_exact)) / np.float32(math.log(1024 / max_exact))
                         * np.float32(nb - max_exact)).astype(np.int32)
    large = np.minimum(large, nb - 1)
    return ret + np.where(n < max_exact, n, large)


def _const_tables(c):
    t = {}
    t["ident"] = np.eye(128, dtype=np.float32)
    rm = np.zeros((128, 128), np.float32)
    for e in range(128):
        rm[e, e + 32 if (e % 64) < 32 else e - 32] = 1.0
    t["rotm"] = rm
    p = np.arange(128)
    half, kc = p // 64, p % 64
    col = np.arange(NA_COLS_T)
    dd, qc = col // 64, col % 64
    d = 18 - (dd[None, :] - half[:, None])
    cs = np.clip(qc - 8, 0, GRID_W - 16)
    okd = (d >= 0) & (d <= 14)
    okc = (kc[:, None] >= cs[None, :]) & (kc[:, None] <= cs[None, :] + 15)
    t["na_cm"] = (okd & okc).astype(np.float32)
    t["_na_d"] = d
    t["_na_dc"] = kc[:, None] - qc[None, :] + 15
    t["_na_ok"] = okd & okc
    rmk = np.zeros((3, 8, 128, NQ), np.float32)
    f = np.arange(NQ)
    for v, g in enumerate([0, 1, c.G - 1]):
        for kbr in range(-2, 6):
            kb = 4 * g + kbr
            if kb < 0 or kb >= c.ROWS // 2:
                continue
            kr = 2 * kb + half
            qr = 8 * g + f // 64
            rs = np.clip(qr - 4, 0, c.ROWS - 8)
            rmk[v, kbr + 2] = ((kr[:, None] >= rs[None, :]) & (kr[:, None] <= rs[None, :] + 7))
    t["na_rm"] = rmk.reshape(24, 128, NQ)
    j = np.arange(DIL_J)
    off = p[:, None] - j[None, :] + 128 * DIL_KBMAX
    a = np.abs(off)
    mult = (a <= 64).astype(np.float32) + ((off % 4 == 0) & (a <= 256)) + ((off % 16 == 0) & (a <= 1024))
    t["dil_mz"] = mult.astype(np.float32)
    t["_dil_bucket"] = _t5_bucket(off)
    return t


def _rope_tables(c, j):
    tpos = np.arange(j * c.TOK, (j + 1) * c.TOK)
    half = HD // 2
    inv = np.exp(np.float32(-math.log(10000.0)) * np.arange(0, half, 2, dtype=np.float32) / np.float32(half))
    e = np.arange(128)
    pos = np.where((e < 64)[:, None], (tpos // GRID_W)[None, :], (tpos % GRID_W)[None, :]).astype(np.float32)
    ang = pos * inv[e % 32][:, None]
    cos = np.cos(ang).astype(np.float32)
    sin = np.sin(ang).astype(np.float32)
    sg = np.where(((e % 64) < 32)[:, None], -sin, sin).astype(np.float32)
    sc = np.float32(ATTN_SCALE)
    return np.stack([cos * sc, sg * sc, cos, sg]).astype(np.float32)


def _vecT(v, KC):
    lead = v.shape[:-1]
    t = v.reshape(-1, KC, 128).transpose(2, 0, 1)
    return np.ascontiguousarray(t).reshape(128, -1)


def prepare_inputs(c, x, cvec, ada_w, ada_b, ada_layer_emb, norm_gains, w_in, w_out, q_gain, k_gain,
                   na_rpb, t5_table, w_mlp_in, w_mlp_out):
    f32 = np.float32
    x = np.asarray(x, f32); cvec = np.asarray(cvec, f32)
    D, KC, GRP = c.D, c.KC, c.GRP
    ct = _const_tables(c)
    cols_qk, cols_v, rows_o = [], [], []
    for j in range(GRP):
        for h in _q_heads(c, j):
            cols_qk.append(np.arange(h * 128, (h + 1) * 128))
        for h in _kv_heads(c, j):
            cols_qk.append(D + np.arange(h * 128, (h + 1) * 128))
        for h in _kv_heads(c, j):
            cols_v.append(D + c.KVW + np.arange(h * 128, (h + 1) * 128))
        for h in _q_heads(c, j):
            rows_o.append(np.arange(h * 128, (h + 1) * 128))
    cols_qk = np.concatenate(cols_qk); cols_v = np.concatenate(cols_v); rows_o = np.concatenate(rows_o)

    def shares(tl, key):
        Dp, R, C = tl.shape
        rp = c.wrp[key]
        v = tl.reshape(Dp, R // (8 * rp), 8, rp, C)
        return [np.ascontiguousarray(v[:, :, r]).reshape(Dp, R // 8, C) for r in range(8)]

    wqk = shares(np.stack([_tileB(np.asarray(w_in[l], f32)[:, cols_qk], KC) for l in range(c.DEPTH)]), "wqk")
    wv = shares(np.stack([_tileV(np.asarray(w_in[l], f32)[:, cols_v], KC) for l in range(c.DEPTH)]), "wv")
    wo = shares(np.stack([_tileB(np.asarray(w_out[l], f32)[rows_o, :], KC) for l in range(c.DEPTH)]), "wo")
    w1 = shares(np.stack([_tileB(np.asarray(w_mlp_in[l], f32), KC) for l in range(c.DEPTH)]), "w1")
    w2 = shares(np.stack([_tileB(np.asarray(w_mlp_out[l], f32), KC) for l in range(c.DEPTH)]), "w2")
    ada_w = np.asarray(ada_w, f32); ada_b = np.asarray(ada_b, f32)
    embT = _vecT(np.asarray(ada_layer_emb, f32), KC)
    gainT = _vecT(np.asarray(norm_gains, f32), KC)
    cT = np.ascontiguousarray(cvec.reshape(c.B, KC, 128).transpose(2, 1, 0)).reshape(128, KC * c.B)
    qkg = np.ascontiguousarray(np.stack([np.asarray(q_gain, f32), np.asarray(k_gain, f32)], 1)
                               .reshape(c.DEPTH * 2, 128).T)
    na_rpb = np.asarray(na_rpb, f32); t5_table = np.asarray(t5_table, f32)
    dcl = np.clip(ct["_na_d"], 0, 14); dcc = np.clip(ct["_na_dc"], 0, 30)
    in_maps = []
    for r in range(8):
        b, j = r // GRP, r % GRP
        m = {}
        m["xT"] = np.ascontiguousarray(x[b, j * c.TOK:(j + 1) * c.TOK, :].T).reshape(KC, 128, c.TOK)
        m["ids"] = np.array([[j, b]], np.int32)
        m["cT"] = cT
        m["adaw"] = np.ascontiguousarray(ada_w[:, r * c.NA6:(r + 1) * c.NA6])
        m["adab"] = np.ascontiguousarray(np.broadcast_to(ada_b[r * c.NA6:(r + 1) * c.NA6][None, :], (c.B, c.NA6)))
        m["embT"] = embT; m["gainT"] = gainT; m["qkg"] = qkg
        m["ident"] = ct["ident"]; m["rotm"] = ct["rotm"]
        m["rope"] = _rope_tables(c, j)
        m["wqk"] = wqk[r]; m["wv"] = wv[r]; m["wo"] = wo[r]; m["w1"] = w1[r]; m["w2"] = w2[r]
        nat = np.zeros((c.DEPTH, 2, 128, NA_COLS_T), f32)
        for l in range(c.DEPTH):
            for hl in range(2):
                g = na_rpb[l, 2 * j + hl][dcl, dcc]
                nat[l, hl] = np.where(ct["_na_ok"], g, 0.0)
        m["na_t"] = nat
        m["na_cm"] = ct["na_cm"]; m["na_rm"] = ct["na_rm"]
        m["dil_z"] = np.ascontiguousarray(
            np.stack([t5_table[:, 3 * j + hl][ct["_dil_bucket"]] for hl in range(3)]).astype(f32))
        m["dil_mz"] = ct["dil_mz"]
        in_maps.append(m)
    return in_maps


class Sem:
    def __init__(self, nc, name):
        self.h = nc.alloc_semaphore(name)
        self.v = 0


class Builder:
    def __init__(self, cfg, debug_outs=()):
        self.c = cfg
        self.nc = bass.Bass("TRN2", target_bir_lowering=False)
        self.dbg = set(debug_outs)
        nc = self.nc
        self.SP, self.ACT, self.DVE, self.POOL, self.PE = nc.sync, nc.scalar, nc.vector, nc.gpsimd, nc.tensor
        self.prog = {}
        for nm, e in [("act", self.ACT), ("dve", self.DVE), ("pe", self.PE), ("pool", self.POOL)]:
            self.prog[id(e)] = Sem(nc, "pg_" + nm)
        self.allsems = list(self.prog.values())
        self.engines = [self.SP, self.ACT, self.DVE, self.POOL, self.PE]
        self._nsem = 0

    def sem(self, name):
        if getattr(self, "sem_pool", None):
            return self.sem_pool.pop()
        s = Sem(self.nc, f"{name}_{self._nsem}")
        self._nsem += 1
        self.allsems.append(s)
        return s

    def done(self, eng, instr):
        s = self.prog[id(eng)]
        instr.then_inc(s.h, 1)
        s.v += 1
        return (s, s.v)

    def dma(self, eng, out, in_, sem):
        eng.dma_start(out=out, in_=in_).then_inc(sem.h, 16)
        sem.v += 16
        return (sem, sem.v)

    def dma_split(self, eng, out, in_, sem, n):
        d0 = out.shape[0]
        assert in_.shape[0] == d0 and d0 % n == 0, (out.shape, in_.shape, n)
        st = d0 // n
        t = None
        for i in range(n):
            t = self.dma(eng, out[i * st:(i + 1) * st], in_[i * st:(i + 1) * st], sem)
        return t

    def wait(self, eng, *tickets):
        for t in tickets:
            if t is None:
                continue
            if isinstance(t, list):
                self.wait(eng, *t)
                continue
            s, v = t
            if v > 0:
                eng.wait_ge(s.h, v)

    def last(self, eng):
        s = self.prog[id(eng)]
        return (s, s.v)

    def barrier(self):
        for e in self.engines:
            for s in self.allsems:
                if s.v > 0:
                    e.wait_ge(s.h, s.v)

    def op(self, eng, fn, *deps, **kw):
        self.wait(eng, self.last(eng), *deps)
        return self.done(eng, fn(**kw))


class Ring:
    def __init__(self, B, name, n, width, dtype):
        self.B = B
        self.n = n
        self.width = width
        self.t = B.stack.enter_context(B.ncp.sbuf_tensor(name, [128, n * width], dtype))
        self.sems = [B.sem(name) for _ in range(n)]
        if not hasattr(B, "sem_pool"):
            B.sem_pool = []
        B.stack.callback(lambda: B.sem_pool.extend(self.sems))
        self.free = [[] for _ in range(n)]
        self.i = -1

    def next(self):
        self.i = (self.i + 1) % self.n
        return self.i

    def ap(self, i, lo=0, hi=None, p0=0, p1=128):
        hi = self.width if hi is None else hi
        return self.t[p0:p1, i * self.width + lo: i * self.width + hi]


def build_program(c, debug_outs=()):
    from contextlib import ExitStack
    B = Builder(c, debug_outs)
    nc = B.nc
    _orig_sbuf = nc.sbuf_tensor
    _uid = [0]

    class _NCProxy:
        def __getattr__(self, k):
            return getattr(nc, k)

        def sbuf_tensor(self, name, shape, dtype):
            _uid[0] += 1
            return _orig_sbuf(f"{name}_u{_uid[0]}", shape, dtype)

    ncp = _NCProxy()
    B.ncp = ncp
    SP, ACT, DVE, POOL, PE = B.SP, B.ACT, B.DVE, B.POOL, B.PE
    KC, TOK, NTG, GRP, FC, DEPTH, S = c.KC, c.TOK, c.NTG, c.GRP, c.FC, c.DEPTH, c.S

    def din(name, shape, dt=F32):
        return nc.dram_tensor(name, list(shape), dt, kind="ExternalInput")

    def dint(name, shape, dt):
        return nc.dram_tensor(name, list(shape), dt)

    R8 = lambda rows: rows // 8
    xT_in = din("xT", [KC, 128, TOK])
    ids = din("ids", [1, 2], I32)
    cT = din("cT", [128, KC * c.B])
    adaw = din("adaw", [c.D, c.NA6])
    adab = din("adab", [c.B, c.NA6])
    embT = din("embT", [128, DEPTH * 6 * KC])
    gainT = din("gainT", [128, DEPTH * 4 * KC])
    qkg = din("qkg", [128, DEPTH * 2])
    ident_d = din("ident", [128, 128])
    rotm_d = din("rotm", [128, 128])
    rope_d = din("rope", [4, 128, TOK])
    wshapes = c.wshapes
    wsh = {k: din(k, [DEPTH, R8(v[0]), v[1]]) for k, v in wshapes.items()}
    na_t = din("na_t", [DEPTH, 2, 128, NA_COLS_T])
    na_cm = din("na_cm", [128, NA_COLS_T])
    na_rm = din("na_rm", [24, 128, NQ])
    dil_z = din("dil_z", [3, 128, DIL_J])
    dil_mz = din("dil_mz", [128, DIL_J])
    outT = nc.dram_tensor("outT", [KC, 128, TOK], F32, kind="ExternalOutput")

    wbf_sh = {k: dint(k + "_bs", [DEPTH, R8(v[0]), v[1]], BF16) for k, v in wshapes.items()}
    wbf = {k: [dint(f"{k}_bf{l}", [v[0], v[1]], BF16) for l in range(DEPTH)] for k, v in wshapes.items()}
    xT = dint("xres", [KC, 128, TOK], F32)
    hT = dint("hT", [KC, 128, TOK], BF16)
    uT = dint("uT", [FC, 128, TOK], BF16)
    raw = dint("rawqk", [GRP * 4, 128, TOK], F32)
    qk_send = dint("qk_send", [GRP * 14 * 128, TOK], BF16)
    qk_all = dint("qk_all", [GRP * 14, GRP * 128, TOK], BF16)
    qk_loc = dint("qk_loc", [14, 128, S], BF16)
    v_send = dint("v_send", [GRP * TOK, 768], BF16)
    VB = 256
    v_all = dint("v_all", [GRP * (TOK // VB), GRP * VB, 768], BF16)
    v_loc = dint("v_loc", [S, 768], BF16)
    o_send = dint("o_send", [GRP * 8 * 128, TOK], BF16)
    o_all = dint("o_all", [GRP * 8, GRP * 128, TOK], BF16)
    o_loc = dint("o_loc", [GRP * 8, 128, TOK], BF16)
    mod_part = dint("mod_part", [c.B, c.NA6], F32)
    mod_all = dint("mod_all", [8, c.B, c.NA6], F32)
    mod_mine = dint("mod_mine", [6 * KC, 128], F32)
    halfbuf = dint("halfbuf", [2, 4 * 512 * 1024 // 2], BF16)
    halfbuf_f = dint("halfbuf_f", [2, 4 * c.B * c.NA6], F32)
    n_pieces = sum((v[0] // 8) // c.wrp[k] for k, v in c.wshapes.items())
    halfbig = dint("halfbig", [n_pieces, 4 * 128 * 1024], BF16)

    dbg_t = {}

    def dbg_out(name, src_ap_fn, shape, dt):
        if name in B.dbg:
            dbg_t[name] = (nc.dram_tensor("dbg_" + name, list(shape), dt, kind="ExternalOutput"), src_ap_fn)

    with ExitStack() as top:
        B.stack = top
        ps_t = top.enter_context(nc.psum_tensor("ps", [128, 8 * 512], F32))

        def bank(b, w=512, p1=128):
            return ps_t[0:p1, b * 512: b * 512 + w]

        vecs = top.enter_context(ncp.sbuf_tensor("vecs", [128, DEPTH * 6 * KC], F32))
        qkg_s = top.enter_context(ncp.sbuf_tensor("qkg_s", [128, DEPTH * 2], F32))
        ones_f = top.enter_context(ncp.sbuf_tensor("ones_f", [128, 128], F32))
        ones_b = top.enter_context(ncp.sbuf_tensor("ones_b", [128, 128], BF16))
        eps_t = top.enter_context(ncp.sbuf_tensor("eps_t", [128, 2], F32))
        ident = top.enter_context(ncp.sbuf_tensor("ident_s", [128, 128], F32))
        rotm = top.enter_context(ncp.sbuf_tensor("rotm_s", [128, 128], F32))
        me_reg = top.enter_context(POOL.register("me_reg"))
        b_reg = top.enter_context(POOL.register("b_reg"))
        misc = B.sem("misc")
        psem = B.sem("poolq")
        ccs = B.sem("cc")

        B.dma(SP, qkg_s[:, :], qkg[:, :], misc)
        B.dma(SP, ident[:, :], ident_d[:, :], misc)
        t_const = B.dma(SP, rotm[:, :], rotm_d[:, :], misc)
        B.op(DVE, lambda: DVE.memset(ones_f[:, :], 1.0))
        B.op(DVE, lambda: DVE.memset(eps_t[:, :], EPS))
        t_ones = B.op(DVE, lambda: DVE.memset(ones_b[:, :], 1.0))
        POOL.reg_load(me_reg, ids[0:1, 0:1])
        POOL.reg_load(b_reg, ids[0:1, 1:2])
        me_v = POOL.snap(me_reg, min_val=0, max_val=GRP - 1)
        b_v = POOL.snap(b_reg, min_val=0, max_val=c.B - 1)

        def pool_store(out, in_, sem_, *deps):
            B.wait(POOL, *deps)
            return B.dma(POOL, out, in_, sem_)

        def collective(in_ap, out_ap):
            POOL.collective_compute("AllGather", ALU.bypass,
                                    replica_groups=[list(range(g * GRP, (g + 1) * GRP)) for g in range(c.B)],
                                    ins=[in_ap], outs=[out_ap]).then_inc(ccs.h, 1)
            ccs.v += 1
            return (ccs, ccs.v)

        G4_ = [[0, 1, 2, 3], [4, 5, 6, 7]]
        G2_ = [[0, 4], [1, 5], [2, 6], [3, 7]]
        half_state = {"i": 0, "t": [None, None]}

        def collective8(in_ap, out_ap, nelem):
            hi = half_state["i"]
            half_state["i"] = 1 - hi
            B.wait(POOL, half_state["t"][hi])
            hap = halfbuf[hi, 0:4 * nelem].opt()
            POOL.collective_compute("AllGather", ALU.bypass, replica_groups=G4_,
                                    ins=[in_ap], outs=[hap]).then_inc(ccs.h, 1)
            ccs.v += 1
            B.wait(POOL, (ccs, ccs.v))
            POOL.collective_compute("AllGather", ALU.bypass, replica_groups=G2_,
                                    ins=[hap], outs=[out_ap]).then_inc(ccs.h, 1)
            ccs.v += 1
            half_state["t"][hi] = (ccs, ccs.v)
            return (ccs, ccs.v)

        w_ready = []
        pending = {l: [] for l in range(DEPTH)}
        with ExitStack() as ph:
            B.stack = ph
            CB = 4096
            stg = Ring(B, "wstg", 3, CB, F32)
            obf = Ring(B, "wobf", 3, CB, BF16)
            ci = 0
            for l in range(DEPTH):
                tks = []
                for k, (Rr, Cc) in wshapes.items():
                    rows = R8(Rr)
                    flat = rows * Cc
                    assert flat % (128 * 128) == 0
                    qn = flat // 128
                    nb = (qn + CB - 1) // CB
                    while qn % nb:
                        nb += 1
                    wcol = qn // nb
                    src = wsh[k][l].rearrange("r c -> (r c)").rearrange("(n p w) -> n p w", p=128, w=wcol)
                    dst = wbf_sh[k][l].rearrange("r c -> (r c)").rearrange("(n p w) -> n p w", p=128, w=wcol)
                    st_t = []
                    for n in range(nb):
                        i = stg.next()
                        B.wait(SP, stg.free[i])
                        tl = B.dma(SP, stg.ap(i, 0, wcol), src[n], stg.sems[i])
                        o = obf.next()
                        eng = DVE if ci % 2 == 0 else ACT
                        ci += 1
                        if eng is DVE:
                            tc_ = B.op(DVE, lambda: DVE.tensor_copy(out=obf.ap(o, 0, wcol), in_=stg.ap(i, 0, wcol)),
                                       tl, obf.free[o])
                        else:
                            tc_ = B.op(ACT, lambda: ACT.activation(out=obf.ap(o, 0, wcol), in_=stg.ap(i, 0, wcol),
                                                                    func=AF.Copy), tl, obf.free[o])
                        stg.free[i] = [tc_]
                        ts = pool_store(dst[n], obf.ap(o, 0, wcol), obf.sems[o], tc_)
                        obf.free[o] = [ts]
                        st_t.append(ts)
                    B.wait(POOL, st_t[-1])
                    for s__ in obf.sems:
                        B.wait(POOL, (s__, s__.v))
                    rp = c.wrp[k]
                    for n in range(rows // rp):
                        assert rp * Cc * 2 <= 512 * 1024
                        if l == 0:
                            tks.append(collective8(wbf_sh[k][l, n * rp:(n + 1) * rp, :].opt(),
                                                   wbf[k][l][n * 8 * rp:(n + 1) * 8 * rp, :].opt(), rp * Cc))
                        else:
                            pending[l].append((wbf_sh[k][l, n * rp:(n + 1) * rp, :].opt(),
                                               wbf[k][l][n * 8 * rp:(n + 1) * 8 * rp, :].opt(), rp * Cc))
                w_ready.append(tks[-1] if tks else None)
        B.barrier()

        with ExitStack() as ph:
            B.stack = ph
            NB = c.B
            ct_s = ph.enter_context(ncp.sbuf_tensor("ct_s", [128, KC * NB], F32))
            sc_s = ph.enter_context(ncp.sbuf_tensor("sc_s", [128, KC * NB], F32))
            ab_s = ph.enter_context(ncp.sbuf_tensor("ab_s", [NB, c.NA6], F32))
            mp_s = ph.enter_context(ncp.sbuf_tensor("mp_s", [NB, c.NA6], F32))
            emb_s = ph.enter_context(ncp.sbuf_tensor("emb_s", [128, DEPTH * 6 * KC], F32))
            gain_s = ph.enter_context(ncp.sbuf_tensor("gain_s", [128, DEPTH * 4 * KC], F32))
            modT = ph.enter_context(ncp.sbuf_tensor("modT", [128, 6 * KC], F32))
            mrow = ph.enter_context(ncp.sbuf_tensor("mrow", [128, 2 * 128], F32))
            tmpv = ph.enter_context(ncp.sbuf_tensor("tmpv", [128, 6 * KC], F32))
            awr = Ring(B, "awr", 3, c.NA6, F32)
            t0 = B.dma(SP, ct_s[:, :], cT[:, :], misc)
            B.dma(SP, ab_s[:, :], adab[:, :], misc)
            B.dma(SP, emb_s[:, :], embT[:, :], misc)
            t_ld = B.dma(SP, gain_s[:, :], gainT[:, :], misc)
            t_sg = B.op(ACT, lambda: ACT.activation(out=sc_s[:, :], in_=ct_s[:, :], func=AF.Sigmoid), t_ld)
            t_sc = B.op(DVE, lambda: DVE.tensor_tensor(out=sc_s[:, :], in0=sc_s[:, :], in1=ct_s[:, :], op=ALU.mult), t_sg)
            nn = c.NA6 // 512
            tmm = None
            for kc in range(KC):
                i = awr.next()
                B.wait(SP, awr.free[i])
                tl = B.dma(SP, awr.ap(i), adaw[kc * 128:(kc + 1) * 128, :], awr.sems[i])
                B.wait(PE, tl, t_sc)
                for n in range(nn):
                    ins = PE.matmul(bank(n, 512, NB), lhsT=sc_s[:, kc * NB:(kc + 1) * NB],
                                    rhs=awr.ap(i, n * 512, (n + 1) * 512), start=(kc == 0), stop=(kc == KC - 1))
                tmm = B.done(PE, ins)
                awr.free[i] = [tmm]
            tl = None
            for n in range(nn):
                tl = B.op(DVE, lambda: DVE.tensor_tensor(out=mp_s[:, n * 512:(n + 1) * 512], in0=bank(n, 512, NB),
                                                         in1=ab_s[:, n * 512:(n + 1) * 512], op=ALU.add), tmm)
            ts = pool_store(mod_part[:, :], mp_s[:, :], psem, tl)
            B.wait(POOL, ts)
            hapf = halfbuf_f[0, :].opt()
            POOL.collective_compute("AllGather", ALU.bypass, replica_groups=G4_,
                                    ins=[mod_part.ap().opt()], outs=[hapf]).then_inc(ccs.h, 1)
            ccs.v += 1
            B.wait(POOL, (ccs, ccs.v))
            POOL.collective_compute("AllGather", ALU.bypass, replica_groups=G2_,
                                    ins=[hapf], outs=[mod_all.ap().opt()]).then_inc(ccs.h, 1)
            ccs.v += 1
            tcc = (ccs, ccs.v)
            B.wait(POOL, tcc)
            tmine = B.dma(POOL, mod_mine.ap().rearrange("k p -> (k p)").rearrange("(r n) -> r n", r=8),
                          mod_all[:, bass.ds(b_v, 1), :].rearrange("r o n -> r (o n)"), psem)
            ntile = (6 * KC + 127) // 128
            rpt = 6 * KC // ntile
            tprev = None
            B.wait(SP, tmine)
            for tI in range(ntile):
                tl = B.dma(SP, mrow[0:rpt, tI * 128:(tI + 1) * 128], mod_mine[tI * rpt:(tI + 1) * rpt, :], misc)
                B.wait(SP, tl)
                B.wait(PE, tl, t_const, tprev)
                tp = B.done(PE, PE.transpose(out=bank(7, rpt), in_=mrow[0:rpt, tI * 128:(tI + 1) * 128],
                                             identity=ident[0:rpt, 0:rpt]))
                tprev = B.op(DVE, lambda: DVE.tensor_copy(out=modT[:, tI * rpt:(tI + 1) * rpt], in_=bank(7, rpt)), tp)
            for l in range(DEPTH):
                vb = l * 6 * KC
                gb = l * 4 * KC
                B.op(DVE, lambda: DVE.tensor_tensor(out=tmpv[:, :], in0=modT[:, :], in1=emb_s[:, vb:vb + 6 * KC],
                                                    op=ALU.add), tprev, t_ld)

                def seg(i):
                    return tmpv[:, i * KC:(i + 1) * KC]

                def vo(i):
                    return vecs[:, vb + i * KC: vb + (i + 1) * KC]

                def gn(i):
                    return gain_s[:, gb + i * KC: gb + (i + 1) * KC]
                B.op(DVE, lambda: DVE.scalar_tensor_tensor(out=vo(0), in0=seg(1), scalar=1.0, in1=gn(0),
                                                           op0=ALU.add, op1=ALU.mult))
                B.op(DVE, lambda: DVE.tensor_copy(out=vo(1), in_=seg(0)))
                B.op(DVE, lambda: DVE.tensor_tensor(out=vo(2), in0=seg(2), in1=gn(1), op=ALU.mult))
                B.op(DVE, lambda: DVE.scalar_tensor_tensor(out=vo(3), in0=seg(4), scalar=1.0, in1=gn(2),
                                                           op0=ALU.add, op1=ALU.mult))
                B.op(DVE, lambda: DVE.tensor_copy(out=vo(4), in_=seg(3)))
                B.op(DVE, lambda: DVE.tensor_tensor(out=vo(5), in0=seg(5), in1=gn(3), op=ALU.mult))
            B.dma_split(SP, xT[:, :, :], xT_in[:, :, :], misc, KC)
        B.barrier()

        def vec(l, i, kc):
            o = l * 6 * KC + i * KC + kc
            return vecs[:, o:o + 1]

        def view_pkt(dram3, t0_, nt):
            return dram3[:, :, t0_:t0_ + nt].rearrange("k p t -> p k t")

        def phase_norm(l, ia, ib):
            NT = 256
            with ExitStack() as ph:
                B.stack = ph
                xr = Ring(B, "nx", 2, KC * NT, F32)
                hr = Ring(B, "nh", 2, KC * NT, BF16)
                acc = ph.enter_context(ncp.sbuf_tensor("nacc", [128, NT], F32))
                tmp = ph.enter_context(ncp.sbuf_tensor("ntmp", [128, NT], F32))
                rstd = ph.enter_context(ncp.sbuf_tensor("nrstd", [128, NT], F32))
                pe_prev = None
                for tI in range(TOK // NT):
                    i = xr.next()
                    B.wait(SP, xr.free[i])
                    tl = B.dma_split(SP, xr.t[:, i * KC * NT:(i + 1) * KC * NT].rearrange("p (k t) -> p k t", k=KC),
                                     view_pkt(xT, tI * NT, NT), xr.sems[i], 4)
                    xs = lambda kc: xr.ap(i, kc * NT, (kc + 1) * NT)
                    B.op(DVE, lambda: DVE.tensor_tensor(out=acc[:, :], in0=xs(0), in1=xs(0), op=ALU.mult), tl, pe_prev)
                    for kc in range(1, KC):
                        B.op(DVE, lambda: DVE.tensor_tensor(out=tmp[:, :], in0=xs(kc), in1=xs(kc), op=ALU.mult))
                        ta = B.op(DVE, lambda: DVE.tensor_tensor(out=acc[:, :], in0=acc[:, :], in1=tmp[:, :], op=ALU.add))
                    B.wait(PE, ta, t_ones)
                    pe_prev = B.done(PE, PE.matmul(bank(7, NT), lhsT=ones_f[:, :], rhs=acc[:, :], start=True, stop=True))
                    tsq = B.op(ACT, lambda: ACT.activation(out=rstd[:, :], in_=bank(7, NT), func=AF.Sqrt, bias=eps_t[:, 0:1],
                                                           scale=1.0 / c.D), pe_prev, B.last(DVE))
                    pe_prev = tsq
                    tr = B.op(DVE, lambda: DVE.reciprocal(out=rstd[:, :], in_=rstd[:, :]), tsq)
                    o = hr.next()
                    ta_ = None
                    for kc in range(KC):
                        td = B.op(DVE, lambda: DVE.tensor_tensor(out=xs(kc), in0=xs(kc), in1=rstd[:, :], op=ALU.mult))
                        ta_ = B.op(ACT, lambda: ACT.activation(out=hr.ap(o, kc * NT, (kc + 1) * NT), in_=xs(kc),
                                                               func=AF.Identity, bias=vec(l, ib, kc), scale=vec(l, ia, kc)),
                                   td, hr.free[o] if kc == 0 else None)
                    xr.free[i] = [ta_]
                    B.wait(POOL, ta_)
                    ts = B.dma_split(POOL, view_pkt(hT, tI * NT, NT),
                                     hr.t[:, o * KC * NT:(o + 1) * KC * NT].rearrange("p (k t) -> p k t", k=KC), hr.sems[o], 4)
                    hr.free[o] = [ts]
            B.barrier()

        def linear(ph, wfull, M, NKG, rhs_view, epilogue, post_tg=None, nbanks=4):
            rr = Ring(B, "lrhs", 2, KC * NQ, BF16)
            wr = Ring(B, "lw", 3, KC * 128, BF16)
            bfree = [[] for _ in range(nbanks)]
            bi = -1
            for tg in range(NTG):
                for kg in range(NKG):
                    r = rr.next()
                    B.wait(SP, rr.free[r])
                    trl = B.dma_split(SP, rr.t[:, r * KC * NQ:(r + 1) * KC * NQ].rearrange("p (k t) -> p k t", k=KC),
                                      rhs_view(kg, tg), rr.sems[r], 4)
                    tmm = None
                    for m in range(M):
                        w = wr.next()
                        B.wait(SP, wr.free[w])
                        row = (m * NKG + kg) * 128
                        twl = B.dma(SP, wr.ap(w), wfull[row:row + 128, :], wr.sems[w])
                        bi = (bi + 1) % nbanks
                        B.wait(PE, twl, trl if m == 0 else None, bfree[bi])
                        for kc in range(KC):
                            ins = PE.matmul(bank(bi), lhsT=wr.ap(w, kc * 128, (kc + 1) * 128),
                                            rhs=rr.ap(r, kc * NQ, (kc + 1) * NQ), start=(kc == 0), stop=(kc == KC - 1))
                        tmm = B.done(PE, ins)
                        wr.free[w] = [tmm]
                        bfree[bi] = [epilogue(m, kg, tg, bank(bi), tmm)]
                    rr.free[r] = [tmm]
                if post_tg is not None:
                    post_tg(tg)

        def phase_qkv(l):
            with ExitStack() as ph:
                B.stack = ph
                B.wait(PE, w_ready[l])
                B.wait(SP, w_ready[l])
                sb = Ring(B, "qstb", 4, NQ, BF16)
                sf = Ring(B, "qstf", 2, NQ, F32)
                gq_idx = {5: 0, 6: 1, 7: 2, 13: 3}

                def epi(m, kg, tg, ps, tmm):
                    j, cc = m // 14, m % 14
                    if cc in gq_idx:
                        i = sf.next()
                        t = B.op(ACT, lambda: ACT.activation(out=sf.ap(i), in_=ps, func=AF.Copy), tmm, sf.free[i])
                        ts = pool_store(raw[j * 4 + gq_idx[cc], :, tg * NQ:(tg + 1) * NQ], sf.ap(i), sf.sems[i], t)
                        sf.free[i] = [ts]
                    else:
                        i = sb.next()
                        scl = ATTN_SCALE if cc < 8 else 1.0
                        t = B.op(ACT, lambda: ACT.activation(out=sb.ap(i), in_=ps, func=AF.Copy, scale=scl), tmm, sb.free[i])
                        ts = pool_store(qk_send[m * 128:(m + 1) * 128, tg * NQ:(tg + 1) * NQ], sb.ap(i), sb.sems[i], t)
                        sb.free[i] = [ts]
                    return t

                linear(ph, wbf["wqk"][l], c.MQK, 1, lambda kg, tg: view_pkt(hT, tg * NQ, NQ), epi)
            B.barrier()
            with ExitStack() as ph:
                B.stack = ph
                rr = Ring(B, "vrhs", 2, KC * NQ, BF16)
                wr = Ring(B, "vw", 2, KC * NV, BF16)
                so = Ring(B, "vst", 4, NV, BF16)
                bfree = [[] for _ in range(4)]
                bi = -1
                for tg in range(NTG):
                    r = rr.next()
                    B.wait(SP, rr.free[r])
                    trl = B.dma_split(SP, rr.t[:, r * KC * NQ:(r + 1) * KC * NQ].rearrange("p (k t) -> p k t", k=KC),
                                      view_pkt(hT, tg * NQ, NQ), rr.sems[r], 4)
                    for vg in range(c.NVG):
                        w = wr.next()
                        B.wait(SP, wr.free[w])
                        twl = B.dma(SP, wr.ap(w), wbf["wv"][l][vg * 128:(vg + 1) * 128, :], wr.sems[w])
                        for tt in range(NQ // 128):
                            bi = (bi + 1) % 4
                            B.wait(PE, twl, trl, bfree[bi])
                            for kc in range(KC):
                                ins = PE.matmul(bank(bi, NV), lhsT=rr.ap(r, kc * NQ + tt * 128, kc * NQ + (tt + 1) * 128),
                                                rhs=wr.ap(w, kc * NV, (kc + 1) * NV), start=(kc == 0), stop=(kc == KC - 1))
                            tmm = B.done(PE, ins)
                            i = so.next()
                            t = B.op(ACT, lambda: ACT.activation(out=so.ap(i), in_=bank(bi, NV), func=AF.Copy), tmm, so.free[i])
                            bfree[bi] = [t]
                            j, hf = vg // 2, vg % 2
                            r0 = j * TOK + tg * NQ + tt * 128
                            ts = pool_store(v_send[r0:r0 + 128, hf * NV:(hf + 1) * NV], so.ap(i), so.sems[i], t)
                            so.free[i] = [ts]
                        wr.free[w] = [tmm]
                    rr.free[r] = [tmm]
            B.barrier()

        def phase_qkn(l):
            with ExitStack() as ph:
                B.stack = ph
                rope = ph.enter_context(ncp.sbuf_tensor("rope_s", [128, 4 * TOK], F32))
                tr_ = B.dma(SP, rope[:, :].rearrange("p (f t) -> p f t", f=4), rope_d[:, :, :].rearrange("f p t -> p f t"), misc)
                rw = Ring(B, "kraw", 2, NQ, F32)
                sq = ph.enter_context(ncp.sbuf_tensor("ksq", [128, NQ], F32))
                rs = ph.enter_context(ncp.sbuf_tensor("krs", [128, NQ], F32))
                xn = ph.enter_context(ncp.sbuf_tensor("kxn", [128, NQ], F32))
                t1 = ph.enter_context(ncp.sbuf_tensor("kt1", [128, NQ], F32))
                t2 = ph.enter_context(ncp.sbuf_tensor("kt2", [128, NQ], F32))
                so = Ring(B, "kst", 2, NQ, BF16)
                tb_prev = None
                tf_prev = None
                for idx in range(GRP * 4):
                    j, wq = idx // 4, idx % 4
                    isq = wq < 3
                    cc = 5 + wq if isq else 13
                    gcol = l * 2 + (0 if isq else 1)
                    for tg in range(NTG):
                        i = rw.next()
                        B.wait(SP, rw.free[i])
                        tl = B.dma(SP, rw.ap(i), raw[idx, :, tg * NQ:(tg + 1) * NQ], rw.sems[i])
                        ta = B.op(DVE, lambda: DVE.tensor_tensor(out=sq[:, :], in0=rw.ap(i), in1=rw.ap(i), op=ALU.mult), tl)
                        B.wait(PE, ta, t_ones, tf_prev)
                        tpa = B.done(PE, PE.matmul(bank(0), lhsT=ones_f[:, :], rhs=sq[:, :], start=True, stop=True))
                        tsq = B.op(ACT, lambda: ACT.activation(out=rs[:, :], in_=bank(0), func=AF.Sqrt, bias=eps_t[:, 0:1],
                                                               scale=1.0 / HD), tpa, B.last(DVE))
                        B.op(DVE, lambda: DVE.reciprocal(out=rs[:, :], in_=rs[:, :]), tsq)
                        tx = B.op(DVE, lambda: DVE.scalar_tensor_tensor(out=xn[:, :], in0=rw.ap(i), scalar=qkg_s[:, gcol:gcol + 1],
                                                                        in1=rs[:, :], op0=ALU.mult, op1=ALU.mult), t_const)
                        rw.free[i] = [tx]
                        B.wait(PE, tx, t_const)
                        tpb = B.done(PE, PE.matmul(bank(1), lhsT=rotm[:, :], rhs=xn[:, :], start=True, stop=True))
                        co = (0 if isq else 2) * TOK + tg * NQ
                        si = (1 if isq else 3) * TOK + tg * NQ
                        B.op(DVE, lambda: DVE.tensor_tensor(out=t1[:, :], in0=xn[:, :], in1=rope[:, co:co + NQ], op=ALU.mult), tr_)
                        tf_prev = B.op(DVE, lambda: DVE.tensor_tensor(out=t2[:, :], in0=bank(1), in1=rope[:, si:si + NQ],
                                                                      op=ALU.mult), tpb)
                        o = so.next()
                        tfin = B.op(DVE, lambda: DVE.tensor_tensor(out=so.ap(o), in0=t1[:, :], in1=t2[:, :], op=ALU.add), so.free[o])
                        m = j * 14 + cc
                        ts = pool_store(qk_send[m * 128:(m + 1) * 128, tg * NQ:(tg + 1) * NQ], so.ap(o), so.sems[o], tfin)
                        so.free[o] = [ts]
            B.barrier()

        def exchange_qkv():
            tl = None
            for q in range(GRP * 14):
                tl = collective(qk_send[q * 128:(q + 1) * 128, :].opt(), qk_all[q].opt())
            nvb = TOK // VB
            for q in range(GRP * nvb):
                tl = collective(v_send[q * VB:(q + 1) * VB, :].opt(), v_all[q].opt())
            B.wait(POOL, tl)
            qa5 = qk_all[:, :, :].rearrange("(j q) (i r) t -> j q i r t", j=GRP, i=GRP)
            va5 = v_all[:, :, :].rearrange("(j q) (i r) c -> j q i r c", j=GRP, i=GRP)
            for i in range(GRP):
                td = B.dma(POOL, qk_loc[:, :, i * TOK:(i + 1) * TOK], qa5[bass.ds(me_v, 1), :, i, :, :], psem)
                B.wait(POOL, td)
                td = B.dma(POOL, v_loc[i * TOK:(i + 1) * TOK, :].rearrange("(tb t) c -> tb t c", t=VB),
                           va5[bass.ds(me_v, 1), :, i, :, :], psem)
                B.wait(POOL, td)
            B.barrier()

        def exchange_o():
            tl = None
            for q in range(GRP * 8):
                tl = collective(o_send[q * 128:(q + 1) * 128, :].opt(), o_all[q].opt())
            B.wait(POOL, tl)
            oa5 = o_all[:, :, :].rearrange("(j q) (i r) t -> j q i r t", j=GRP, i=GRP)
            for i in range(GRP):
                td = B.dma(POOL, o_loc[i * 8:(i + 1) * 8, :, :], oa5[bass.ds(me_v, 1), :, i, :, :], psem)
                B.wait(POOL, td)
            B.barrier()

        def phase_att(l):
            G = c.G
            NKB = S // 128
            with ExitStack() as ph:
                B.stack = ph
                kr_ = Ring(B, "ak", 2, S, BF16)
                vr_ = Ring(B, "av", 2, S, BF16)
                qr_ = Ring(B, "aq", 3, NQ, BF16)
                pr_ = Ring(B, "ap", 4, NQ, BF16)
                er_ = Ring(B, "ae", 2, NQ, F32)
                osr = Ring(B, "ao", 2, NQ, BF16)
                rden = ph.enter_context(ncp.sbuf_tensor("arden", [128, NQ], F32))
                etab = Ring(B, "aet", 2, 24 * NQ, BF16)
                tstg = ph.enter_context(ncp.sbuf_tensor("atstg", [128, DIL_J], F32))
                tmsk = ph.enter_context(ncp.sbuf_tensor("atmsk", [128, DIL_J], F32))
                narm = ph.enter_context(ncp.sbuf_tensor("anarm", [128, 24 * NQ], BF16))
                etn = ph.enter_context(ncp.sbuf_tensor("aetn", [128, NA_COLS_T], BF16))
                tprev = None
                for q in range(24):
                    tl = B.dma(SP, tstg[:, 0:NQ], na_rm[q, :, :], misc)
                    B.wait(SP, tl)
                    tprev = B.op(DVE, lambda: DVE.tensor_copy(out=narm[:, q * NQ:(q + 1) * NQ], in_=tstg[:, 0:NQ]), tl)
                    B.wait(SP, tprev)
                sbank_free = [[] for _ in range(3)]
                obank_free = [[] for _ in range(2)]
                sb_i = -1
                ob_i = -1
                kv_cur = {}
                heads = [("na", 8, 0), ("na", 9, 1), ("dil", 10, 2), ("dil", 11, 3), ("dil", 12, 4),
                         ("gqa", 13, 5), ("gqa", 13, 5), ("gqa", 13, 5)]
                tk = tv = None
                ks = vs = None
                for hq, (typ, kch, vb) in enumerate(heads):
                    if kv_cur.get("k") != kch:
                        ks = kr_.next()
                        B.wait(SP, kr_.free[ks])
                        tk = B.dma(SP, kr_.ap(ks), qk_loc[kch, :, :], kr_.sems[ks])
                        vs = vr_.next()
                        B.wait(SP, vr_.free[vs])
                        tv = B.dma_split(SP, vr_.ap(vs).rearrange("p (n c) -> p n c", c=128),
                                         v_loc[:, vb * 128:(vb + 1) * 128].rearrange("(n p) c -> p n c", p=128), vr_.sems[vs], 8)
                        kv_cur["k"] = kch
                    te = None
                    es = None
                    if typ == "na":
                        es = etab.next()
                        B.wait(SP, tprev)
                        tl1 = B.dma(SP, tstg[:, 0:NA_COLS_T], na_t[l, hq, :, :], misc)
                        tl2 = B.dma(SP, tmsk[:, 0:NA_COLS_T], na_cm[:, :], misc)
                        ta = B.op(ACT, lambda: ACT.activation(out=tstg[:, 0:NA_COLS_T], in_=tstg[:, 0:NA_COLS_T], func=AF.Exp), tl1, tl2)
                        tprev = B.op(DVE, lambda: DVE.tensor_tensor(out=etn[:, :], in0=tstg[:, 0:NA_COLS_T],
                                                                    in1=tmsk[:, 0:NA_COLS_T], op=ALU.mult), ta, tl2)
                        for v in range(3):
                            for kbr in range(-2, 6):
                                q = v * 8 + kbr + 2
                                dd0 = 11 - 2 * kbr
                                te = B.op(DVE, lambda: DVE.tensor_tensor(out=etab.ap(es, q * NQ, (q + 1) * NQ),
                                                                         in0=etn[:, dd0 * 64: dd0 * 64 + NQ],
                                                                         in1=narm[:, q * NQ:(q + 1) * NQ], op=ALU.mult),
                                          etab.free[es] if q == 0 else None)
                        tprev = te
                    elif typ == "dil":
                        es = etab.next()
                        B.wait(SP, tprev)
                        tl1 = B.dma(SP, tstg[:, :], dil_z[hq - 2, :, :], misc)
                        tl2 = B.dma(SP, tmsk[:, :], dil_mz[:, :], misc)
                        ta = B.op(ACT, lambda: ACT.activation(out=tstg[:, :], in_=tstg[:, :], func=AF.Exp), tl1, tl2)
                        te = B.op(DVE, lambda: DVE.tensor_tensor(out=etab.ap(es, 0, DIL_J), in0=tstg[:, :], in1=tmsk[:, :],
                                                                 op=ALU.mult), ta, tl2, etab.free[es])
                        tprev = te
                    last_pv = None
                    for g in range(G):
                        if typ == "na":
                            v = 0 if g == 0 else (2 if g == G - 1 else 1)
                            kbs = [(kb, etab.ap(es, (v * 8 + kb - 4 * g + 2) * NQ, (v * 8 + kb - 4 * g + 3) * NQ))
                                   for kb in range(max(0, 4 * g - 2), min(NKB - 1, 4 * g + 5) + 1)]
                        elif typ == "dil":
                            kbs = []
                            for kb in range(max(0, 4 * g + DIL_KBMIN), min(NKB - 1, 4 * g + DIL_KBMAX) + 1):
                                o_ = 128 * (DIL_KBMAX - (kb - 4 * g))
                                kbs.append((kb, etab.ap(es, o_, o_ + NQ)))
                        else:
                            kbs = [(kb, None) for kb in range(NKB)]
                        n = len(kbs)
                        qs = qr_.next()
                        B.wait(SP, qr_.free[qs])
                        tq = B.dma(SP, qr_.ap(qs), qk_loc[hq, :, g * NQ:(g + 1) * NQ], qr_.sems[qs])
                        ob_i = (ob_i + 1) % 2
                        po, pd = bank(4 + ob_i), bank(6 + ob_i)
                        LA = 2
                        tP = [None] * n
                        pslot = [None] * n
                        for step in range(n + LA):
                            if step < n:
                                kb, eap = kbs[step]
                                sb_i = (sb_i + 1) % 3
                                B.wait(PE, tq, tk, sbank_free[sb_i])
                                tS = B.done(PE, PE.matmul(bank(sb_i), lhsT=kr_.ap(ks, kb * 128, (kb + 1) * 128),
                                                          rhs=qr_.ap(qs), start=True, stop=True))
                                p_ = pr_.next()
                                pslot[step] = p_
                                if eap is None:
                                    tP[step] = B.op(ACT, lambda: ACT.activation(out=pr_.ap(p_), in_=bank(sb_i), func=AF.Exp),
                                                    tS, pr_.free[p_])
                                    sbank_free[sb_i] = [tP[step]]
                                else:
                                    e_ = er_.next()
                                    tE = B.op(ACT, lambda: ACT.activation(out=er_.ap(e_), in_=bank(sb_i), func=AF.Exp),
                                              tS, er_.free[e_])
                                    sbank_free[sb_i] = [tE]
                                    tP[step] = B.op(DVE, lambda: DVE.tensor_tensor(out=pr_.ap(p_), in0=er_.ap(e_), in1=eap,
                                                                                   op=ALU.mult), tE, te, pr_.free[p_])
                                    er_.free[e_] = [tP[step]]
                            if step >= LA:
                                i = step - LA
                                kb = kbs[i][0]
                                B.wait(PE, tP[i], tv, t_ones, obank_free[ob_i] if i == 0 else None)
                                PE.matmul(po, lhsT=vr_.ap(vs, kb * 128, (kb + 1) * 128), rhs=pr_.ap(pslot[i]),
                                          start=(i == 0), stop=(i == n - 1))
                                last_pv = B.done(PE, PE.matmul(pd, lhsT=ones_b[:, :], rhs=pr_.ap(pslot[i]),
                                                               start=(i == 0), stop=(i == n - 1)))
                                pr_.free[pslot[i]] = [last_pv]
                        qr_.free[qs] = [last_pv]
                        B.op(DVE, lambda: DVE.reciprocal(out=rden[:, :], in_=pd), last_pv)
                        o = osr.next()
                        tf = B.op(DVE, lambda: DVE.tensor_tensor(out=osr.ap(o), in0=po, in1=rden[:, :], op=ALU.mult), osr.free[o])
                        obank_free[ob_i] = [tf]
                        dest, gl = g // NTG, g % NTG
                        r0 = (dest * 8 + hq) * 128
                        ts = pool_store(o_send[r0:r0 + 128, gl * NQ:(gl + 1) * NQ], osr.ap(o), osr.sems[o], tf)
                        osr.free[o] = [ts]
                    kr_.free[ks] = [last_pv]
                    vr_.free[vs] = [last_pv]
                    if es is not None:
                        etab.free[es] = [B.last(DVE)]
            B.barrier()

        def resid_linear(l, wkey, M, NKG, rhs_view, gi):
            with ExitStack() as ph:
                B.stack = ph
                B.wait(PE, w_ready[l])
                B.wait(SP, w_ready[l])
                ybuf = ph.enter_context(ncp.sbuf_tensor("ybuf", [128, KC * NQ], F32))
                acc = ph.enter_context(ncp.sbuf_tensor("racc", [128, NQ], F32))
                tmp = ph.enter_context(ncp.sbuf_tensor("rtmp", [128, NQ], F32))
                rstd = ph.enter_context(ncp.sbuf_tensor("rrstd", [128, NQ], F32))
                xo = Ring(B, "rxo", 3, NQ, F32)
                xos = Ring(B, "rxos", 3, 2, F32)
                yfree = [None] * M
                state = {"pe7": None}

                def yb(m):
                    return ybuf[:, m * NQ:(m + 1) * NQ]

                def epi(m, kg, tg, ps, tmm):
                    if kg == 0:
                        t = B.op(DVE, lambda: DVE.tensor_copy(out=yb(m), in_=ps), tmm, yfree[m])
                    else:
                        t = B.op(DVE, lambda: DVE.tensor_tensor(out=yb(m), in0=yb(m), in1=ps, op=ALU.add), tmm)
                    if kg == NKG - 1:
                        if m == 0:
                            B.op(DVE, lambda: DVE.tensor_tensor(out=acc[:, :], in0=yb(m), in1=yb(m), op=ALU.mult), state["pe7"])
                        else:
                            B.op(DVE, lambda: DVE.tensor_tensor(out=tmp[:, :], in0=yb(m), in1=yb(m), op=ALU.mult))
                            B.op(DVE, lambda: DVE.tensor_tensor(out=acc[:, :], in0=acc[:, :], in1=tmp[:, :], op=ALU.add))
                    return t

                def post(tg):
                    ta = B.last(DVE)
                    B.wait(PE, ta, t_ones)
                    tp = B.done(PE, PE.matmul(bank(7), lhsT=ones_f[:, :], rhs=acc[:, :], start=True, stop=True))
                    state["pe7"] = tp
                    tsq = B.op(ACT, lambda: ACT.activation(out=rstd[:, :], in_=bank(7), func=AF.Sqrt, bias=eps_t[:, 0:1],
                                                           scale=1.0 / c.D), tp, B.last(DVE))
                    state["pe7"] = tsq
                    B.op(DVE, lambda: DVE.reciprocal(out=rstd[:, :], in_=rstd[:, :]), tsq)
                    for m in range(M):
                        i = xo.next()
                        B.wait(SP, xo.free[i])
                        tl = B.dma(SP, xo.ap(i), xT[m, :, tg * NQ:(tg + 1) * NQ], xo.sems[i])
                        B.op(DVE, lambda: DVE.tensor_tensor(out=yb(m), in0=yb(m), in1=rstd[:, :], op=ALU.mult))
                        tx = B.op(DVE, lambda: DVE.scalar_tensor_tensor(out=xo.ap(i), in0=yb(m), scalar=vec(l, gi, m),
                                                                        in1=xo.ap(i), op0=ALU.mult, op1=ALU.add), tl)
                        yfree[m] = tx
                        ts = pool_store(xT[m, :, tg * NQ:(tg + 1) * NQ], xo.ap(i), xos.sems[i], tx)
                        xo.free[i] = [ts]

                linear(ph, wbf[wkey][l], M, NKG, rhs_view, epi, post)
            B.barrier()

        def phase_mlp1(l):
            with ExitStack() as ph:
                B.stack = ph
                sf = Ring(B, "m1f", 2, NQ, F32)
                sb = Ring(B, "m1b", 3, NQ, BF16)

                def epi(m, kg, tg, ps, tmm):
                    i = sf.next()
                    t = B.op(ACT, lambda: ACT.activation(out=sf.ap(i), in_=ps, func=AF.Relu), tmm, sf.free[i])
                    o = sb.next()
                    t2 = B.op(DVE, lambda: DVE.tensor_tensor(out=sb.ap(o), in0=sf.ap(i), in1=sf.ap(i), op=ALU.mult), t, sb.free[o])
                    sf.free[i] = [t2]
                    ts = pool_store(uT[m, :, tg * NQ:(tg + 1) * NQ], sb.ap(o), sb.sems[o], t2)
                    sb.free[o] = [ts]
                    return t

                linear(ph, wbf["w1"][l], FC, 1, lambda kg, tg: view_pkt(hT, tg * NQ, NQ), epi)
            B.barrier()

        def bg_step(l, b):
            nl = l + 1
            if nl >= DEPTH:
                return
            P = pending[nl]
            nbt = 9
            per = (len(P) + nbt - 1) // nbt
            if b >= 2:
                k = b - 2
                for idx in range(k * per, min(len(P), (k + 1) * per)):
                    in_ap, out_ap, nelem = P[idx]
                    POOL.collective_compute("AllGather", ALU.bypass, replica_groups=G2_,
                                            ins=[halfbig[idx, 0:4 * nelem].opt()], outs=[out_ap]).then_inc(ccs.h, 1)
                    ccs.v += 1
                if b == 10:
                    w_ready[nl] = (ccs, ccs.v)
            if b <= 9:
                k = b - 1
                for idx in range(k * per, min(len(P), (k + 1) * per)):
                    in_ap, out_ap, nelem = P[idx]
                    POOL.collective_compute("AllGather", ALU.bypass, replica_groups=G4_,
                                            ins=[in_ap], outs=[halfbig[idx, 0:4 * nelem].opt()]).then_inc(ccs.h, 1)
                    ccs.v += 1

        stop = getattr(c, "stop", None)
        for l in range(DEPTH):
            steps = [
                ("norm1", lambda: phase_norm(l, 0, 1)),
                ("qkv", lambda: phase_qkv(l)),
                ("qkn", lambda: phase_qkn(l)),
                ("xq", exchange_qkv),
                ("att", lambda: phase_att(l)),
                ("xo", exchange_o),
                ("wo", lambda: resid_linear(l, "wo", KC, 1, lambda kg, tg: view_pkt(o_loc, tg * NQ, NQ), 2)),
                ("norm2", lambda: phase_norm(l, 3, 4)),
                ("mlp1", lambda: phase_mlp1(l)),
                ("mlp2", lambda: resid_linear(
                    l, "w2", KC, c.NKG2,
                    lambda kg, tg: uT[kg * KC:(kg + 1) * KC, :, tg * NQ:(tg + 1) * NQ].rearrange("k p t -> p k t"), 5)),
            ]
            halted = False
            for bidx, (nm, fn) in enumerate(steps):
                fn()
                bg_step(l, bidx + 1)
                if stop == nm:
                    halted = True
                    break
            if halted:
                break

        dbg_src = {"hT": hT, "qk_loc": qk_loc, "v_loc": v_loc, "o_loc": o_loc, "uT": uT, "raw": raw,
                   "qk_send": qk_send, "xres": xT, "mod_mine": mod_mine}
        fin = B.sem("fin")
        for nm in B.dbg:
            src = dbg_src[nm]
            dt_ = nc.dram_tensor("dbg_" + nm, list(src.shape), src.dtype, kind="ExternalOutput")
            B.dma_split(SP, dt_.ap(), src.ap(), fin, src.shape[0])
        B.dma_split(SP, outT[:, :, :], xT[:, :, :], fin, KC)
        B.wait(SP, (fin, fin.v))
    return nc


def run(c, inputs, debug_outs=()):
    in_maps = prepare_inputs(c, **inputs)
    nc = build_program(c, debug_outs)
    res = run_bass_kernel_spmd(nc, in_maps, core_ids=list(range(8)))
    out = np.zeros((c.B, c.S, c.D), np.float32)
    for r in range(8):
        b, j = r // c.GRP, r % c.GRP
        o = np.asarray(res.results[r]["outT"]).reshape(c.D, c.TOK)
        out[b, j * c.TOK:(j + 1) * c.TOK, :] = o.T
    return out, res


def kernel(x, c, ada_w, ada_b, ada_layer_emb, norm_gains, w_in, w_out, q_gain, k_gain,
           na_rpb, t5_table, w_mlp_in, w_mlp_out):
    inputs = dict(x=x, cvec=c, ada_w=ada_w, ada_b=ada_b, ada_layer_emb=ada_layer_emb, norm_gains=norm_gains,
                  w_in=w_in, w_out=w_out, q_gain=q_gain, k_gain=k_gain, na_rpb=na_rpb, t5_table=t5_table,
                  w_mlp_in=w_mlp_in, w_mlp_out=w_mlp_out)
    out, _ = run(FULL, inputs)
    return out
```
